# Optimizing a Trainium2 kernel written in Bass

```python
import math
import jax, jax.numpy as jnp
from jax import lax
import numpy as np

D_MODEL = 4096
BATCH = 4
SEQ = 4096
DEPTH = 2

N_MIXERS = 2
N_A_LAYERS = (DEPTH + 1) // 2
N_B_LAYERS = DEPTH // 2

A_HEADS = 32
A_DK = D_MODEL // A_HEADS
A_DV = D_MODEL // A_HEADS
A_QK_W = A_HEADS * A_DK
A_V_W = A_HEADS * A_DV
A_CONV_CH = 2 * A_QK_W + A_V_W
A_CONV = 4
A_CHUNK = 64
A_IN_W = A_CONV_CH + A_V_W + 2 * A_HEADS

B_HEADS = 16
B_DH = D_MODEL // (2 * B_HEADS)
B_Q_W = 2 * B_HEADS * B_DH
B_V_W = 2 * B_HEADS * B_DH
B_IN_W = 2 * B_Q_W + 2 * B_V_W
Q_BLOCK = 128

REL_BUCKETS = 32
REL_MAX_DIST = 128

DEEPNORM_ALPHA = (2.0 * DEPTH) ** 0.25
DEEPNORM_BETA = (8.0 * DEPTH) ** -0.25

LN_EPS = 1e-5
RMS_EPS = 1e-6
L2_EPS = 1e-6

kernel_name = "hybrid_deltanet_diffattn_deepnorm"


def layer_norm(x, g, b):
    xf = x.astype(jnp.float32)
    mu = jnp.mean(xf, axis=-1, keepdims=True)
    var = jnp.mean(jnp.square(xf - mu), axis=-1, keepdims=True)
    y = (xf - mu) * lax.rsqrt(var + LN_EPS)
    return (y * g.astype(jnp.float32) + b.astype(jnp.float32)).astype(x.dtype)


def rms_norm(x, w, eps):
    xf = x.astype(jnp.float32)
    y = xf * lax.rsqrt(jnp.mean(jnp.square(xf), axis=-1, keepdims=True) + eps)
    return y * w.astype(jnp.float32)


def l2norm(x):
    return x * lax.rsqrt(jnp.sum(jnp.square(x), axis=-1, keepdims=True) + L2_EPS)


def causal_short_conv(x, w):
    T = x.shape[1]
    K = w.shape[-1]
    xp = jnp.pad(x, ((0, 0), (K - 1, 0), (0, 0)))
    y = xp[:, 0:T, :] * w[:, 0]
    for j in range(1, K):
        y = y + xp[:, j:j + T, :] * w[:, j]
    return y


def gated_delta_rule_chunked(q, k, v, g, beta):
    Bn, H, T, dk = q.shape
    dv = v.shape[-1]
    C = A_CHUNK
    N = T // C
    q = q.reshape(Bn, H, N, C, dk)
    k = k.reshape(Bn, H, N, C, dk)
    v = v.reshape(Bn, H, N, C, dv)
    g = g.reshape(Bn, H, N, C)
    beta = beta.reshape(Bn, H, N, C)

    gc = jnp.cumsum(g, axis=-1)
    k_beta = k * beta[..., None]
    v_beta = v * beta[..., None]
    idx = jnp.arange(C)
    lower_incl = idx[:, None] >= idx[None, :]
    strict = idx[:, None] > idx[None, :]
    diff = gc[..., :, None] - gc[..., None, :]
    decay_mask = jnp.exp(jnp.where(lower_incl, diff, -jnp.inf))

    L = jnp.where(strict, jnp.einsum('bhncd,bhnsd->bhncs', k_beta, k) * decay_mask, 0.0)
    eye = jnp.eye(C, dtype=q.dtype)
    T_mat = lax.linalg.triangular_solve(eye + L, jnp.broadcast_to(eye, L.shape),
                                        left_side=True, lower=True, unit_diagonal=True)
    w = jnp.einsum('bhncs,bhnsd->bhncd', T_mat, k_beta * jnp.exp(gc)[..., None])
    u = jnp.einsum('bhncs,bhnse->bhnce', T_mat, v_beta)
    attn_intra = jnp.einsum('bhncd,bhnsd->bhncs', q, k) * decay_mask
    q_decay = q * jnp.exp(gc)[..., None]
    k_state = k * jnp.exp(gc[..., -1:] - gc)[..., None]
    chunk_decay = jnp.exp(gc[..., -1])

    xs = (jnp.moveaxis(w, 2, 0), jnp.moveaxis(u, 2, 0), jnp.moveaxis(attn_intra, 2, 0),
          jnp.moveaxis(q_decay, 2, 0), jnp.moveaxis(k_state, 2, 0), jnp.moveaxis(chunk_decay, 2, 0))

    def step(S, inp):
        w_c, u_c, a_c, qd_c, ks_c, cd_c = inp
        v_new = u_c - jnp.einsum('bhcd,bhde->bhce', w_c, S)
        o = jnp.einsum('bhcd,bhde->bhce', qd_c, S) + jnp.einsum('bhcs,bhse->bhce', a_c, v_new)
        S = S * cd_c[..., None, None] + jnp.einsum('bhcd,bhce->bhde', ks_c, v_new)
        return S, o

    S0 = jnp.zeros((Bn, H, dk, dv), dtype=q.dtype)
    _, outs = lax.scan(step, S0, xs)
    return jnp.moveaxis(outs, 0, 2).reshape(Bn, H, T, dv)


def gated_deltanet_branch(x, w_in, conv_w, a_log, dt_bias, norm_w, w_out):
    Bn, T, _ = x.shape
    h = x @ w_in
    qkv, z, a, b = jnp.split(h, [A_CONV_CH, A_CONV_CH + A_V_W, A_CONV_CH + A_V_W + A_HEADS], axis=-1)
    qkv = jax.nn.silu(causal_short_conv(qkv, conv_w)).astype(jnp.float32)
    q, k, v = jnp.split(qkv, [A_QK_W, 2 * A_QK_W], axis=-1)
    q = l2norm(q.reshape(Bn, T, A_HEADS, A_DK)) * (A_DK ** -0.5)
    k = l2norm(k.reshape(Bn, T, A_HEADS, A_DK))
    v = v.reshape(Bn, T, A_HEADS, A_DV)
    beta = jax.nn.sigmoid(b.astype(jnp.float32))
    g = -jnp.exp(a_log.astype(jnp.float32)) * jax.nn.softplus(
        a.astype(jnp.float32) + dt_bias.astype(jnp.float32))
    o = gated_delta_rule_chunked(q.transpose(0, 2, 1, 3), k.transpose(0, 2, 1, 3),
                                 v.transpose(0, 2, 1, 3), g.transpose(0, 2, 1),
                                 beta.transpose(0, 2, 1))
    o = o.transpose(0, 2, 1, 3)
    gate = jax.nn.silu(z.astype(jnp.float32)).reshape(Bn, T, A_HEADS, A_DV)
    o = rms_norm(o, norm_w, RMS_EPS) * gate
    return o.reshape(Bn, T, A_V_W).astype(x.dtype) @ w_out


def t5_causal_bucket(rel):
    n = jnp.maximum(rel, 0)
    max_exact = REL_BUCKETS // 2
    nf = jnp.maximum(n, 1).astype(jnp.float32)
    large = max_exact + (jnp.log(nf / max_exact) / math.log(REL_MAX_DIST / max_exact)
                         * (REL_BUCKETS - max_exact)).astype(jnp.int32)
    large = jnp.minimum(large, REL_BUCKETS - 1)
    return jnp.where(n < max_exact, n, large)


def lambda_init_fn(layer_idx):
    return 0.8 - 0.6 * math.exp(-0.3 * layer_idx)


def diff_attention_branch(x, w_in, lam_q1, lam_k1, lam_q2, lam_k2, subln_w, rel_bias, w_out, layer_idx):
    Bn, T, _ = x.shape
    lam_init = lambda_init_fn(layer_idx)
    h = x @ w_in
    q, k, v, z = jnp.split(h, [B_Q_W, 2 * B_Q_W, 2 * B_Q_W + B_V_W], axis=-1)
    q = q.reshape(Bn, T, B_HEADS, 2, B_DH).transpose(0, 2, 3, 1, 4)
    k = k.reshape(Bn, T, B_HEADS, 2, B_DH).transpose(0, 2, 3, 1, 4)
    v = v.reshape(Bn, T, B_HEADS, 2 * B_DH).transpose(0, 2, 1, 3)
    lam = (jnp.exp(jnp.sum(lam_q1.astype(jnp.float32) * lam_k1.astype(jnp.float32)))
           - jnp.exp(jnp.sum(lam_q2.astype(jnp.float32) * lam_k2.astype(jnp.float32))) + lam_init)
    scale = B_DH ** -0.5
    nb = T // Q_BLOCK
    q_blocks = jnp.moveaxis(q.reshape(Bn, B_HEADS, 2, nb, Q_BLOCK, B_DH), 3, 0)
    starts = jnp.arange(nb, dtype=jnp.int32) * Q_BLOCK
    kpos = jnp.arange(T, dtype=jnp.int32)

    def attend(args):
        qb, start = args
        s = jnp.einsum('bhpqd,bhpkd->bhpqk', qb, k).astype(jnp.float32) * scale
        qpos = start + jnp.arange(Q_BLOCK, dtype=jnp.int32)
        rel = qpos[:, None] - kpos[None, :]
        bias = rel_bias.astype(jnp.float32)[t5_causal_bucket(rel)]
        s = s + bias.transpose(2, 0, 1)[None, :, None]
        s = jnp.where((rel >= 0)[None, None, None], s, -jnp.inf)
        p = jax.nn.softmax(s, axis=-1)
        a = p[:, :, 0] - lam * p[:, :, 1]
        return jnp.einsum('bhqk,bhkd->bhqd', a.astype(v.dtype), v)

    o = lax.map(attend, (q_blocks, starts))
    o = o.transpose(1, 0, 3, 2, 4).reshape(Bn, T, B_HEADS, 2 * B_DH)
    gate = jax.nn.silu(z.astype(jnp.float32)).reshape(Bn, T, B_HEADS, 2 * B_DH)
    o = rms_norm(o, subln_w, 1e-5) * (1.0 - lam_init) * gate
    return o.reshape(Bn, T, B_V_W).astype(x.dtype) @ w_out


def setup_inputs(seed: int = 0) -> dict:
    key = jax.random.key(seed)
    ks = jax.random.split(key, 20)
    f32 = jnp.float32
    x = jax.random.normal(ks[0], (BATCH, SEQ, D_MODEL), f32)
    ln_g = 1.0 + 0.02 * jax.random.normal(ks[1], (DEPTH, D_MODEL), f32)
    ln_b = 0.02 * jax.random.normal(ks[2], (DEPTH, D_MODEL), f32)
    rel_bias = 0.5 * jax.random.normal(ks[3], (REL_BUCKETS, B_HEADS), f32)

    a_col_scale = jnp.ones((A_IN_W,), f32).at[2 * A_QK_W:A_CONV_CH].set(DEEPNORM_BETA)
    a_w_in = jax.random.normal(ks[4], (N_A_LAYERS, D_MODEL, A_IN_W), f32) * (D_MODEL ** -0.5) * a_col_scale
    a_conv_w = jax.random.normal(ks[5], (N_A_LAYERS, A_CONV_CH, A_CONV), f32) * (A_CONV ** -0.5)
    a_a_log = jnp.log(jax.random.uniform(ks[6], (N_A_LAYERS, A_HEADS), f32, 1.0, 16.0))
    dt = jnp.exp(jax.random.uniform(ks[7], (N_A_LAYERS, A_HEADS), f32, math.log(1e-3), math.log(1e-1)))
    a_dt_bias = dt + jnp.log(-jnp.expm1(-dt))
    a_norm_w = 1.0 + 0.02 * jax.random.normal(ks[8], (N_A_LAYERS, A_DV), f32)
    a_w_out = jax.random.normal(ks[9], (N_A_LAYERS, A_V_W, D_MODEL), f32) * (A_V_W ** -0.5) * DEEPNORM_BETA

    b_col_scale = jnp.ones((B_IN_W,), f32).at[2 * B_Q_W:2 * B_Q_W + B_V_W].set(DEEPNORM_BETA)
    b_w_in = jax.random.normal(ks[10], (N_B_LAYERS, D_MODEL, B_IN_W), f32) * (D_MODEL ** -0.5) * b_col_scale
    b_lam_q1 = 0.1 * jax.random.normal(ks[11], (N_B_LAYERS, B_DH), f32)
    b_lam_k1 = 0.1 * jax.random.normal(ks[12], (N_B_LAYERS, B_DH), f32)
    b_lam_q2 = 0.1 * jax.random.normal(ks[13], (N_B_LAYERS, B_DH), f32)
    b_lam_k2 = 0.1 * jax.random.normal(ks[14], (N_B_LAYERS, B_DH), f32)
    b_subln_w = 1.0 + 0.02 * jax.random.normal(ks[15], (N_B_LAYERS, 2 * B_DH), f32)
    b_w_out = jax.random.normal(ks[16], (N_B_LAYERS, B_V_W, D_MODEL), f32) * (B_V_W ** -0.5) * DEEPNORM_BETA
    return {"x": x, "ln_g": ln_g, "ln_b": ln_b, "rel_bias": rel_bias,
            "a_w_in": a_w_in, "a_conv_w": a_conv_w, "a_a_log": a_a_log, "a_dt_bias": a_dt_bias,
            "a_norm_w": a_norm_w, "a_w_out": a_w_out,
            "b_w_in": b_w_in, "b_lam_q1": b_lam_q1, "b_lam_k1": b_lam_k1, "b_lam_q2": b_lam_q2,
            "b_lam_k2": b_lam_k2, "b_subln_w": b_subln_w, "b_w_out": b_w_out}


def reference(x, ln_g, ln_b, rel_bias, a_w_in, a_conv_w, a_a_log, a_dt_bias, a_norm_w, a_w_out,
              b_w_in, b_lam_q1, b_lam_k1, b_lam_q2, b_lam_k2, b_subln_w, b_w_out):
    for i in range(DEPTH):
        j = i // N_MIXERS
        if i % N_MIXERS == 0:
            y = gated_deltanet_branch(x, a_w_in[j], a_conv_w[j], a_a_log[j], a_dt_bias[j],
                                      a_norm_w[j], a_w_out[j])
        else:
            y = diff_attention_branch(x, b_w_in[j], b_lam_q1[j], b_lam_k1[j], b_lam_q2[j], b_lam_k2[j],
                                      b_subln_w[j], rel_bias, b_w_out[j], i)
        x = layer_norm(DEEPNORM_ALPHA * x + y, ln_g[i], ln_b[i])
    return x
```

```python
import numpy as np
import concourse.bass as bass
import concourse.mybir as mybir
from concourse.bass_utils import run_bass_kernel_spmd

F32 = mybir.dt.float32
BF16 = mybir.dt.bfloat16
AF = mybir.ActivationFunctionType
ALU = mybir.AluOpType

SEM_EPOCH = 30000
NEGBIG = -30000.0


class Sched:
    ENGS = ("pe", "act", "dve", "pool", "sp")

    def __init__(self, nc, n_dma_sems=12):
        self.nc = nc
        self.streams = {e: [] for e in self.ENGS}
        self.seq = {e: 0 for e in self.ENGS}
        self.res = {}
        self.obs = {e: {} for e in self.ENGS}
        self.sems = {}
        self.semkeys = []
        self.dma_cnt = {}
        self.dma_rr = {e: 0 for e in self.ENGS}
        self.n_dma_sems = n_dma_sems
        self.final_waits = []

    def _semkey(self, k):
        if k not in self.sems:
            self.sems[k] = None
            self.semkeys.append(k)
        return k

    def _need(self, eng, dep, waits):
        if dep is None:
            return
        k, v = dep
        if k[0] == "eng" and k[1] == eng and eng == "pe":
            return
        if k[0] == "eng":
            for (kk, vv) in list(self.obs[eng].items()):
                if kk[0] == "eng" and kk[1] == k[1] and kk[2] > k[2]:
                    return
        if self.obs[eng].get(k, 0) >= v:
            return
        self.obs[eng][k] = v
        waits.append((k, v))

    def _deps(self, eng, reads, writes):
        waits = []
        for r in reads:
            ent = self.res.get(r)
            if ent is not None:
                self._need(eng, ent[0], waits)
        for w in writes:
            ent = self.res.get(w)
            if ent is not None:
                self._need(eng, ent[0], waits)
                for dk, dv in ent[1].items():
                    self._need(eng, (dk, dv), waits)
        return waits

    def _commit(self, dep, reads, writes):
        for r in reads:
            ent = self.res.setdefault(r, [None, {}])
            if ent[1].get(dep[0], 0) < dep[1]:
                ent[1][dep[0]] = dep[1]
        for w in writes:
            self.res[w] = [dep, {}]

    def op(self, eng, fn, reads=(), writes=()):
        waits = self._deps(eng, reads, writes)
        self.seq[eng] += 1
        n = self.seq[eng]
        k = self._semkey(("eng", eng, (n - 1) // SEM_EPOCH))
        v = (n - 1) % SEM_EPOCH + 1
        self.streams[eng].append((waits, fn, (k, v, 1)))
        self._commit((k, v), reads, writes)

    def dma(self, eng, out, in_, reads=(), writes=(), final=False):
        waits = self._deps(eng, reads, writes)
        i = self.dma_rr[eng]
        self.dma_rr[eng] = (i + 1) % self.n_dma_sems
        k = self._semkey(("dma", eng, i))
        prev = self.dma_cnt.get(k, 0)
        if prev:
            self._need(eng, (k, prev), waits)
        v = prev + 16
        if v > SEM_EPOCH:
            raise RuntimeError("dma sem overflow")
        self.dma_cnt[k] = v
        self.streams[eng].append((waits, lambda e, o=out, s=in_: e.dma_start(out=o, in_=s), (k, v, 16)))
        self._commit((k, v), reads, writes)
        if final:
            self.final_waits.append((k, v))

    def emit(self):
        nc = self.nc
        from contextlib import ExitStack
        with ExitStack() as es:
            for k in self.semkeys:
                self.sems[k] = es.enter_context(nc.semaphore("s_" + "_".join(str(x) for x in k)))
            block = es.enter_context(nc.Block())
            fw = {}
            for (k, v) in self.final_waits:
                fw[k] = max(fw.get(k, 0), v)
            sems = self.sems

            def mk(engname):
                stream = self.streams[engname]

                def body(e):
                    for (waits, fn, inc) in stream:
                        for (k, v) in waits:
                            e.wait_ge(sems[k], v)
                        ins = fn(e)
                        k, v, step = inc
                        ins.then_inc(sems[k], step)
                    if engname == "sp":
                        for k, v in fw.items():
                            e.wait_ge(sems[k], v)
                return body

            block.tensor(mk("pe"))
            block.scalar(mk("act"))
            block.vector(mk("dve"))
            block.gpsimd(mk("pool"))
            block.sync(mk("sp"))


class Pools:
    def __init__(self, nc, es):
        self.nc, self.es, self.n = nc, es, 0

    def sb(self, shape, dt, name=None):
        self.n += 1
        return self.es.enter_context(self.nc.sbuf_tensor(f"{name or 'sb'}_{self.n}", list(shape), dt))

    def ps(self, shape, dt=F32, name=None):
        self.n += 1
        return self.es.enter_context(self.nc.psum_tensor(f"{name or 'ps'}_{self.n}", list(shape), dt))


def phase_gemm_fm(S, A, PS, ident, x_dram, w_dram, out_dram, T, D, NCOL, TT, tag):
    KC = D // 128
    TT = min(TT, T)
    NQ = 512 if TT >= 512 else TT
    xstage = [A.alloc([D]) for _ in range(2)]
    xT = A.alloc([KC, TT], BF16)
    CW = 512
    wb = [A.alloc([KC, CW], BF16) for _ in range(2)]
    ost = [A.alloc([NQ]) for _ in range(3)]
    tp = [PS[i].rearrange("p (a b) -> p a b", a=4) for i in range(2)]
    mm = [PS[2 + i] for i in range(3)]
    wv = w_dram.rearrange("(kc p) c -> p kc c", p=128)
    n_tp = 0
    n_w = 0
    n_mm = 0
    for tt in range(T // TT):
        for tb in range(TT // 128):
            si = (tt * (TT // 128) + tb) % 2
            xs = xstage[si]
            t0 = tt * TT + tb * 128
            S.dma("sp", xs, x_dram[t0:t0 + 128, :], reads=[(tag, "x", t0 // 128)], writes=[("xst", si)])
            for g in range((KC + 3) // 4):
                pi = n_tp % 2
                pt = tp[pi]
                nj = min(4, KC - 4 * g)
                for j in range(nj):
                    kc = 4 * g + j
                    TR(S, pt[:, j, :], xs[:, kc * 128:(kc + 1) * 128], ident, [("xst", si)], [("tp", pi)])
                dst = xT[:, 4 * g:4 * g + nj, tb * 128:(tb + 1) * 128]
                CP(S, "act" if n_tp % 2 == 0 else "dve", dst, pt[:, 0:nj, :], [("tp", pi)],
                   [("xT", kc2, tb) for kc2 in range(4 * g, 4 * g + nj)])
                n_tp += 1
        for cg in range((NCOL + CW - 1) // CW):
            cw = min(CW, NCOL - cg * CW)
            wi = n_w % 2
            w = wb[wi]
            n_w += 1
            S.dma("pool", w[:, :, 0:cw], wv[:, :, cg * CW:cg * CW + cw], reads=[], writes=[("wb", wi)])
            for cb in range((cw + 127) // 128):
                M = min(128, cw - cb * 128)
                for tq in range(TT // NQ):
                    mi = n_mm % 3
                    pm = mm[mi]
                    os_ = ost[mi]
                    for kc in range(KC):
                        MM(S, pm[0:M, 0:NQ], w[:, kc, cb * 128:cb * 128 + M], xT[:, kc, tq * NQ:(tq + 1) * NQ],
                           [("wb", wi)] + [("xT", kc, tb) for tb in range(tq * NQ // 128, (tq + 1) * NQ // 128)],
                           [("mm", mi)], start=(kc == 0), stop=(kc == KC - 1))
                    CP(S, "act" if n_mm % 2 == 0 else "dve", os_[0:M, :], pm[0:M, 0:NQ], [("mm", mi)], [("ost", mi)])
                    r0 = cg * CW + cb * 128
                    c0 = tt * TT + tq * NQ
                    S.dma("sp", out_dram[r0:r0 + M, c0:c0 + NQ], os_[0:M, :], reads=[("ost", mi)],
                          writes=[(tag, "hT", r0 // 128, c0 // 128 + i) for i in range(NQ // 128)])
                    n_mm += 1


class Arena:
    def __init__(self, nc, es, words):
        self.t = es.enter_context(nc.sbuf_tensor("arena", [128, words], F32))
        self.words = words
        self.off = 0

    def reset(self, off=0):
        self.off = off

    def alloc(self, shape, dt=F32):
        n = int(np.prod(shape))
        w = n if dt == F32 else (n + 1) // 2
        w = (w + 7) // 8 * 8
        if self.off + w > self.words:
            raise RuntimeError(f"arena overflow: need {self.off + w} words of {self.words}")
        ap = self.t[:, self.off:self.off + w]
        self.off += w
        if dt != F32:
            ap = ap.bitcast(dt)
        ap = ap[:, 0:n]
        if len(shape) == 2:
            ap = ap.rearrange("p (a b) -> p a b", a=shape[0])
        elif len(shape) == 3:
            ap = ap.rearrange("p (a b c) -> p a b c", a=shape[0], b=shape[1])
        return ap


def _barrier(S):
    deps = []
    for e in S.ENGS:
        n = S.seq[e]
        if n:
            deps.append((("eng", e, (n - 1) // SEM_EPOCH), (n - 1) % SEM_EPOCH + 1))
    for k, v in S.dma_cnt.items():
        deps.append((k, v))
    S.res["__barrier__"] = [None, dict(deps)]
    S.barrier_on = True


_orig_deps = Sched._deps


def _deps_with_barrier(self, eng, reads, writes):
    waits = _orig_deps(self, eng, reads, writes)
    ent = self.res.get("__barrier__")
    if ent is not None:
        for dk, dv in ent[1].items():
            self._need(eng, (dk, dv), waits)
    return waits


Sched._deps = _deps_with_barrier
Sched.barrier = _barrier


def MM(S, out, lhsT, rhs, r, w, start=True, stop=True):
    S.op("pe", lambda e: e.matmul(out, lhsT=lhsT, rhs=rhs, start=start, stop=stop), reads=r, writes=w)


def TR(S, out, in_, ident, r, w):
    S.op("pe", lambda e: e.transpose(out, in_, ident), reads=list(r) + ["ident"], writes=w)


def ACT(S, out, in_, func, r, w, **kw):
    S.op("act", lambda e: e.activation(out=out, in_=in_, func=func, **kw), reads=r, writes=w)


def TT(S, eng, out, a, b, op, r, w):
    S.op(eng, lambda e: e.tensor_tensor(out=out, in0=a, in1=b, op=op), reads=r, writes=w)


def TS(S, eng, out, a, s1, op0, r, w, s2=None, op1=None):
    if op1 is None:
        S.op(eng, lambda e: e.tensor_scalar(out=out, in0=a, scalar1=s1, scalar2=None, op0=op0), reads=r, writes=w)
    else:
        S.op(eng, lambda e: e.tensor_scalar(out=out, in0=a, scalar1=s1, scalar2=s2, op0=op0, op1=op1), reads=r, writes=w)


def STT(S, out, a, s, b, op0, op1, r, w):
    S.op("dve", lambda e: e.scalar_tensor_tensor(out=out, in0=a, scalar=s, in1=b, op0=op0, op1=op1), reads=r, writes=w)


def CP(S, eng, out, in_, r, w):
    if eng == "act":
        S.op("act", lambda e: e.copy(out, in_), reads=r, writes=w)
    else:
        S.op(eng, lambda e: e.tensor_copy(out=out, in_=in_), reads=r, writes=w)


def load_consts(S, A, cdram):
    names = ["ident", "U", "NEG", "SU", "ones", "ones128", "onesm"]
    t = A.alloc([len(names) * 128], F32)
    S.dma("sp", t, cdram[:, :], writes=names)
    return {n: t[:, i * 128:(i + 1) * 128] for i, n in enumerate(names)}


def host_consts():
    i = np.arange(128)
    ident = np.eye(128, dtype=np.float32)
    U = (i[:, None] <= i[None, :]).astype(np.float32)
    NEG = np.where(i[None, :] < i[:, None], NEGBIG, 0.0).astype(np.float32)
    SU = (i[None, :] > i[:, None]).astype(np.float32)
    ones = np.ones((128, 128), np.float32)
    return np.concatenate([ident, U, NEG, SU, ones, ones * 128.0, ones / 128.0], axis=1)


def phase_deltanet(S, A, PS, C, hT, convw_d, alog_d, dtb_d, normw_d, out_d, T, NH, tag):
    NCH = T // 128
    NN = NCH * NH
    assert NN <= 512
    ident, U, NEG, SU, ones, ones128, onesm = (C[n] for n in ["ident", "U", "NEG", "SU", "ones", "ones128", "onesm"])
    raw = [A.alloc([3 + T]) for _ in range(3)]
    acc = [A.alloc([T]) for _ in range(3)]
    zt = A.alloc([T])
    oT = A.alloc([T], BF16)
    cw = A.alloc([NH * 12])
    nw = A.alloc([1])
    alog = A.alloc([NN]); dtb = A.alloc([NN])
    a_tok = A.alloc([NN]); b_tok = A.alloc([NN]); g = A.alloc([NN]); beta = A.alloc([NN]); gc = A.alloc([NN])
    negc = A.alloc([NN]); negegc = A.alloc([NN]); kdec = A.alloc([NN]); cd = A.alloc([NN]); tmpn = A.alloc([NN])
    sm = {n: A.alloc([128]) for n in ["Kst", "Vt", "Ug", "EGB", "ET", "AT", "tN", "N", "NT", "QdT", "M0", "M1", "MT0", "MT1",
                                      "P0", "P1", "R", "Vn", "sq", "ro", "to", "S", "rt"]}
    rtmp = A.alloc([512])
    sqt = raw[0][:, 3:3 + T]
    abT = raw[1][:, 3:3 + T]

    for x in range(3):
        S.op("pool", lambda e, o=raw[x][:, 0:3]: e.memset(o, 0.0), writes=[("rawpad", x)])
    S.dma("sp", cw, convw_d[:, :], writes=["cw"])
    S.dma("sp", nw, normw_d[:, :], writes=["nw"])
    S.dma("sp", alog, alog_d[:, :], writes=["alog"])
    S.dma("sp", dtb, dtb_d[:, :], writes=["dtb"])
    S.dma("sp", abT[0:2 * NH, :], hT[NH * 512:NH * 512 + 2 * NH, :], reads=[(tag, "hT", NH * 4, c) for c in range(NCH)], writes=[("raw", 1)])
    CPB = 512 // (2 * NH)
    a3 = a_tok.rearrange("p (c h) -> p c h", c=NCH)
    b3 = b_tok.rearrange("p (c h) -> p c h", c=NCH)
    for half in range((NCH + CPB - 1) // CPB):
        pst = PS[6 + half]
        c0 = half * CPB
        nc_ = min(CPB, NCH - c0)
        for c in range(c0, c0 + nc_):
            TR(S, pst[:, (c - c0) * 2 * NH:(c - c0 + 1) * 2 * NH], abT[0:2 * NH, c * 128:(c + 1) * 128], ident[0:2 * NH, 0:2 * NH],
               [("raw", 1)], ["ps%d" % (6 + half)])
        pv = pst[:, 0:nc_ * 2 * NH].rearrange("p (c t h) -> p c t h", c=nc_, t=2)
        CP(S, "dve", a3[:, c0:c0 + nc_, :], pv[:, :, 0, :], ["ps%d" % (6 + half)], ["a_tok"])
        CP(S, "dve", b3[:, c0:c0 + nc_, :], pv[:, :, 1, :], ["ps%d" % (6 + half)], ["b_tok"])
    TT(S, "dve", a_tok, a_tok, dtb, ALU.add, ["a_tok", "dtb"], ["a_tok"])
    ACT(S, a_tok, a_tok, AF.Exp, ["a_tok"], ["a_tok"])
    ACT(S, a_tok, a_tok, AF.Ln, ["a_tok"], ["a_tok"], bias=1.0)
    ACT(S, alog, alog, AF.Exp, ["alog"], ["alog"])
    STT(S, g, a_tok, -1.0, alog, ALU.mult, ALU.mult, ["a_tok", "alog"], ["g"])
    ACT(S, b_tok, b_tok, AF.Exp, ["b_tok"], ["b_tok"], scale=-1.0)
    TS(S, "dve", b_tok, b_tok, 1.0, ALU.add, ["b_tok"], ["b_tok"])
    S.op("dve", lambda e: e.reciprocal(beta, b_tok), reads=["b_tok"], writes=["beta"])
    MM(S, PS[5][:, 0:NN], U, g, ["U", "g"], ["ps5"])
    MM(S, PS[4][:, 0:NN], ones, g, ["ones", "g"], ["ps4"])
    CP(S, "dve", gc, PS[5][:, 0:NN], ["ps5"], ["gc"])
    TS(S, "dve", negc, gc, -1.0, ALU.mult, ["gc"], ["negc"])
    ACT(S, negegc, gc, AF.Exp, ["gc"], ["negegc"])
    TS(S, "dve", negegc, negegc, -1.0, ALU.mult, ["negegc"], ["negegc"])
    TT(S, "dve", tmpn, PS[4][:, 0:NN], gc, ALU.subtract, ["ps4", "gc"], ["tmpn"])
    ACT(S, kdec, tmpn, AF.Exp, ["tmpn"], ["kdec"])
    ACT(S, cd, PS[4][:, 0:NN], AF.Exp, ["ps4"], ["cd"])

    for h in range(NH):
        for x in range(3):
            S.dma("sp", raw[x][:, 3:3 + T], hT[h * 512 + x * 128:h * 512 + (x + 1) * 128, :],
                  reads=[(tag, "hT", h * 4 + x, c) for c in range(NCH)], writes=[("raw", x)])
        S.dma("sp", zt, hT[h * 512 + 384:h * 512 + 512, :], reads=[(tag, "hT", h * 4 + 3, c) for c in range(NCH)], writes=["zt"])
        for x in range(3):
            wq = lambda j: cw[:, h * 12 + x * 4 + j:h * 12 + x * 4 + j + 1]
            TS(S, "dve", acc[x], raw[x][:, 3:3 + T], wq(3), ALU.mult, [("raw", x), "cw"], [("acc", x)])
            for j in (2, 1, 0):
                STT(S, acc[x], raw[x][:, j:j + T], wq(j), acc[x], ALU.mult, ALU.add, [("raw", x), ("rawpad", x), "cw", ("acc", x)], [("acc", x)])
            ACT(S, acc[x], acc[x], AF.Silu, [("acc", x)], [("acc", x)])
        ACT(S, zt, zt, AF.Silu, ["zt"], ["zt"])
        for x in range(2):
            ACT(S, sqt, acc[x], AF.Square, [("acc", x)], [("raw", 0)])
            om = ones128 if x == 0 else ones
            ep = 128.0 * 1e-6 if x == 0 else 1e-6
            for blk in range((T + 511) // 512):
                n = min(512, T - blk * 512)
                sl = slice(blk * 512, blk * 512 + n)
                MM(S, PS[5][:, 0:n], om, sqt[:, sl], ["ones", "ones128", ("raw", 0)], ["ps5"])
                ACT(S, rtmp[:, 0:n], PS[5][:, 0:n], AF.Ln, ["ps5"], ["rtmp"], bias=ep)
                ACT(S, rtmp[:, 0:n], rtmp[:, 0:n], AF.Exp, ["rtmp"], ["rtmp"], scale=-0.5)
                TT(S, "dve", acc[x][:, sl], acc[x][:, sl], rtmp[:, 0:n], ALU.mult, [("acc", x), "rtmp"], [("acc", x)])
        qT, kT, vT = acc
        S.op("pool", lambda e: e.memset(sm["S"], 0.0), writes=["S"])
        for c in range(NCH):
            cs = slice(c * 128, (c + 1) * 128)
            n = c * NH + h
            col = lambda t: t[:, n:n + 1]
            TR(S, PS[0][:, 0:128], kT[:, cs], ident, [("acc", 1)], ["ps0"])
            TR(S, PS[0][:, 128:256], vT[:, cs], ident, [("acc", 2)], ["ps0"])
            ACT(S, sm["Kst"], PS[0][:, 0:128], AF.Copy, ["ps0", "kdec"], ["Kst"], scale=col(kdec))
            CP(S, "dve", sm["Vt"], PS[0][:, 128:256], ["ps0"], ["Vt"])
            MM(S, PS[1][:, 0:128], kT[:, cs], kT[:, cs], [("acc", 1)], ["ps1"])
            MM(S, PS[1][:, 128:256], kT[:, cs], qT[:, cs], [("acc", 1), ("acc", 0)], ["ps1"])
            TS(S, "pool", sm["Ug"], U, col(g), ALU.mult, ["U", "g"], ["Ug"])
            MM(S, PS[1][:, 256:384], ones, sm["Ug"], ["ones", "Ug"], ["ps1"])
            MM(S, PS[1][:, 384:512], ones, sm["Ug"], ["ones", "Ug"], ["ps1"], start=True, stop=False)
            MM(S, PS[1][:, 384:512], ident, NEG, ["ident", "NEG"], ["ps1"], start=False, stop=True)
            ACT(S, sm["EGB"], PS[1][:, 256:384], AF.Exp, ["ps1"], ["EGB"])
            ACT(S, sm["ET"], PS[1][:, 384:512], AF.Exp, ["ps1", "negc"], ["ET"], bias=col(negc))
            TT(S, "dve", sm["AT"], PS[1][:, 128:256], sm["ET"], ALU.mult, ["ps1", "ET"], ["AT"])
            STT(S, sm["tN"], PS[1][:, 0:128], col(beta), sm["ET"], ALU.mult, ALU.mult, ["ps1", "beta", "ET"], ["tN"])
            TT(S, "pool", sm["N"], sm["tN"], SU, ALU.mult, ["tN", "SU"], ["N"])
            TT(S, "pool", sm["QdT"], qT[:, cs], sm["EGB"], ALU.mult, [("acc", 0), "EGB"], ["QdT"])
            TR(S, PS[0][:, 256:384], sm["N"], ident, ["N"], ["ps0"])
            CP(S, "act", sm["NT"], PS[0][:, 256:384], ["ps0"], ["NT"])
            TT(S, "pool", sm["P0"], ident, sm["N"], ALU.subtract, ["ident", "N"], ["P0"])
            Mc, MTc, Pc = sm["N"], sm["NT"], sm["P0"]
            Mk, MTk, Pk = "N", "NT", "P0"
            for lvl in range(1, 7):
                Mn, MTn, Pn = sm[f"M{lvl % 2}"], sm[f"MT{lvl % 2}"], sm[f"P{lvl % 2}"]
                Mnk, MTnk, Pnk = f"M{lvl % 2}", f"MT{lvl % 2}", f"P{lvl % 2}"
                MM(S, PS[2][:, 0:128], Mc, MTc, [Mk, MTk], ["ps2"])
                CP(S, "act", MTn, PS[2][:, 0:128], ["ps2"], [MTnk])
                if lvl < 6:
                    MM(S, PS[3][:, 0:128], MTc, Mc, [Mk, MTk], ["ps3"])
                    CP(S, "dve", Mn, PS[3][:, 0:128], ["ps3"], [Mnk])
                MM(S, PS[4][:, 0:128], MTn, Pc, [MTnk, Pk], ["ps4"])
                TT(S, "dve", Pn, PS[4][:, 0:128], Pc, ALU.add, ["ps4", Pk], [Pnk])
                Mc, MTc, Pc, Mk, MTk, Pk = Mn, MTn, Pn, Mnk, MTnk, Pnk
            Wm, Wk = Pc, Pk
            MM(S, PS[5][:, 0:128], kT[:, cs], sm["S"], [("acc", 1), "S"], ["ps5"])
            STT(S, sm["R"], PS[5][:, 0:128], col(negegc), sm["Vt"], ALU.mult, ALU.add, ["ps5", "negegc", "Vt"], ["R"])
            MM(S, PS[6][:, 0:128], Wm, sm["R"], [Wk, "R"], ["ps6"])
            ACT(S, sm["Vn"], PS[6][:, 0:128], AF.Copy, ["ps6", "beta"], ["Vn"], scale=col(beta))
            MM(S, PS[7][:, 0:128], sm["S"], sm["QdT"], ["S", "QdT"], ["ps7"], start=True, stop=False)
            MM(S, PS[7][:, 0:128], sm["Vn"], sm["AT"], ["Vn", "AT"], ["ps7"], start=False, stop=True)
            MM(S, PS[6][:, 128:256], sm["Kst"], sm["Vn"], ["Kst", "Vn"], ["ps6"])
            STT(S, sm["S"], sm["S"], col(cd), PS[6][:, 128:256], ALU.mult, ALU.add, ["S", "cd", "ps6"], ["S"])
            ACT(S, sm["sq"], PS[7][:, 0:128], AF.Square, ["ps7"], ["sq"])
            MM(S, PS[7][:, 128:256], onesm, sm["sq"], ["onesm", "sq"], ["ps7"])
            ACT(S, sm["ro"], PS[7][:, 128:256], AF.Ln, ["ps7"], ["ro"], bias=1e-6)
            ACT(S, sm["ro"], sm["ro"], AF.Exp, ["ro"], ["ro"], scale=-0.5)
            TT(S, "dve", sm["to"], PS[7][:, 0:128], sm["ro"], ALU.mult, ["ps7", "ro"], ["to"])
            STT(S, oT[:, cs], sm["to"], nw[:, 0:1], zt[:, cs], ALU.mult, ALU.mult, ["to", "nw", "zt"], [("oT", c)])
        S.dma("sp", out_d[h * 128:(h + 1) * 128, :], oT, reads=[("oT", c) for c in range(NCH)], writes=[(tag, "oT", h)], final=True)


ARENA_WORDS = 51200


def _setup(nc, es):
    A = Arena(nc, es, ARENA_WORDS)
    PS = [es.enter_context(nc.psum_tensor(f"psb{i}", [128, 512], F32)) for i in range(8)]
    PS = [p[:, :] for p in PS]
    S = Sched(nc)
    return A, PS, S


def build_l0ab(T, D, NH, TT=1024):
    from contextlib import ExitStack
    nc = bass.Bass("TRN2", target_bir_lowering=False)
    NCOL = NH * 512 + 2 * NH
    NN = (T // 128) * NH
    x = nc.dram_tensor("x", [T, D], F32, kind="ExternalInput").ap()
    w = nc.dram_tensor("w", [D, NCOL], F32, kind="ExternalInput").ap()
    cst = nc.dram_tensor("cst", [128, 7 * 128], F32, kind="ExternalInput").ap()
    convw = nc.dram_tensor("convw", [128, NH * 12], F32, kind="ExternalInput").ap()
    alog = nc.dram_tensor("alog", [128, NN], F32, kind="ExternalInput").ap()
    dtb = nc.dram_tensor("dtb", [128, NN], F32, kind="ExternalInput").ap()
    normw = nc.dram_tensor("normw", [128, 1], F32, kind="ExternalInput").ap()
    out = nc.dram_tensor("oT", [NH * 128, T], BF16, kind="ExternalOutput").ap()
    hT = nc.dram_tensor("hT", [NCOL, T], F32).ap()
    with ExitStack() as es:
        A, PS, S = _setup(nc, es)
        C = load_consts(S, A, cst)
        base = A.off
        phase_gemm_fm(S, A, PS, C["ident"], x, w, hT, T, D, NCOL, TT, "l0")
        S.barrier()
        A.reset(base)
        phase_deltanet(S, A, PS, C, hT, convw, alog, dtb, normw, out, T, NH, "l0")
        S.emit()
    return nc


def prep_l0(inp, b, heads, T):
    H = 32
    QK = H * 128
    CONV = 3 * QK
    w_in = inp["a_w_in"][0]
    cols = []
    for h in heads:
        for base in (0, QK, 2 * QK, CONV):
            cols.append(np.arange(base + h * 128, base + (h + 1) * 128))
    cols.append(np.array([CONV + QK + h for h in heads]))
    cols.append(np.array([CONV + QK + H + h for h in heads]))
    cols = np.concatenate(cols)
    w = np.ascontiguousarray(w_in[:, cols])
    cwf = inp["a_conv_w"][0]
    convw = np.zeros((128, len(heads), 3, 4), np.float32)
    for i, h in enumerate(heads):
        for xx, base in enumerate((0, QK, 2 * QK)):
            convw[:, i, xx, :] = cwf[base + h * 128:base + (h + 1) * 128, :]
    NCH = T // 128
    al = np.tile(inp["a_a_log"][0][heads][None, :], (NCH, 1)).reshape(1, -1)
    db = np.tile(inp["a_dt_bias"][0][heads][None, :], (NCH, 1)).reshape(1, -1)
    return {"x": np.ascontiguousarray(inp["x"][b]), "w": w, "cst": host_consts(),
            "convw": np.ascontiguousarray(convw.reshape(128, -1)),
            "alog": np.ascontiguousarray(np.broadcast_to(al, (128, al.shape[1])).astype(np.float32)),
            "dtb": np.ascontiguousarray(np.broadcast_to(db, (128, db.shape[1])).astype(np.float32)),
            "normw": np.ascontiguousarray(inp["a_norm_w"][0].reshape(128, 1))}


DEEPNORM_ALPHA = float((2.0 * 2) ** 0.25)
LN_EPS = 1e-5


def phase_outproj_ln(S, A, PS, oT_d, w_d, xres_d, lng_d, lnb_d, out_d, TOK, DF, D, tag, final=True, TT2=512):
    KC = DF // 128
    TT2 = min(TT2, TOK)
    NB = TT2 // 128
    CW = 512
    NCG = D // CW
    oT = A.alloc([KC, TT2], BF16)
    z = A.alloc([NB, D])
    wb = [A.alloc([KC, CW], BF16) for _ in range(2)]
    lng = A.alloc([D]); lnb = A.alloc([D])
    xr = [A.alloc([CW]) for _ in range(2)]
    stats = A.alloc([NB, NCG * 6])
    mv = A.alloc([NB, 2])
    rs = A.alloc([NB, 1])
    S.dma("sp", lng, lng_d[:, :], writes=["lng"])
    S.dma("sp", lnb, lnb_d[:, :], writes=["lnb"])
    ov = oT_d.rearrange("(kc p) t -> p kc t", p=128)
    wv = w_d.rearrange("(kc p) c -> p kc c", p=128)
    n_w = 0
    n_mm = 0
    for tt in range(TOK // TT2):
        t0 = tt * TT2
        S.dma("sp", oT, ov[:, :, t0:t0 + TT2], reads=[(tag, "oTin", t0 // 128 + i) for i in range(NB)], writes=["oTt"])
        for cg in range(NCG):
            wi = n_w % 2
            w = wb[wi]
            n_w += 1
            S.dma("pool", w, wv[:, :, cg * CW:(cg + 1) * CW], writes=[("wb", wi)])
            for tb in range(NB):
                bi = n_mm % 4
                pm = PS[bi]
                xi = n_mm % 2
                S.dma("sp", xr[xi], xres_d[t0 + tb * 128:t0 + (tb + 1) * 128, cg * CW:(cg + 1) * CW],
                      reads=[(tag, "xres", (t0 // 128 + tb))], writes=[("xr", xi)])
                for kc in range(KC):
                    MM(S, pm, oT[:, kc, tb * 128:(tb + 1) * 128], w[:, kc, :], ["oTt", ("wb", wi)], [("mm", bi)], start=(kc == 0), stop=(kc == KC - 1))
                zc = z[:, tb, cg * CW:(cg + 1) * CW]
                STT(S, zc, xr[xi], DEEPNORM_ALPHA, pm, ALU.mult, ALU.add, [("xr", xi), ("mm", bi)], [("z", tb, cg)])
                S.op("dve", lambda e, o=stats[:, tb, cg * 6:(cg + 1) * 6], i=zc: e.bn_stats(o, i), reads=[("z", tb, cg)], writes=[("stats", tb)])
                n_mm += 1
        for tb in range(NB):
            zk = [("z", tb, cg) for cg in range(NCG)]
            S.op("dve", lambda e, o=mv[:, tb, :], i=stats[:, tb, :]: e.bn_aggr(o, i), reads=[("stats", tb)], writes=[("mv", tb)])
            ACT(S, rs[:, tb, :], mv[:, tb, 1:2], AF.Ln, [("mv", tb)], [("rs", tb)], bias=LN_EPS)
            ACT(S, rs[:, tb, :], rs[:, tb, :], AF.Exp, [("rs", tb)], [("rs", tb)], scale=-0.5)
            TS(S, "dve", z[:, tb, :], z[:, tb, :], mv[:, tb, 0:1], ALU.subtract, zk + [("mv", tb), ("rs", tb)], zk, s2=rs[:, tb, :], op1=ALU.mult)
            TT(S, "pool", z[:, tb, :], z[:, tb, :], lng, ALU.mult, zk + ["lng"], zk)
            TT(S, "pool", z[:, tb, :], z[:, tb, :], lnb, ALU.add, zk + ["lnb"], zk)
            S.dma("sp", out_d[t0 + tb * 128:t0 + (tb + 1) * 128, :], z[:, tb, :], reads=zk, writes=[(tag, "xout", t0 // 128 + tb)], final=final)


def build_c(TOK, DF, D, TT2=512):
    from contextlib import ExitStack
    nc = bass.Bass("TRN2", target_bir_lowering=False)
    oT = nc.dram_tensor("oT", [DF, TOK], BF16, kind="ExternalInput").ap()
    w = nc.dram_tensor("w", [DF, D], F32, kind="ExternalInput").ap()
    xres = nc.dram_tensor("xres", [TOK, D], F32, kind="ExternalInput").ap()
    lng = nc.dram_tensor("lng", [128, D], F32, kind="ExternalInput").ap()
    lnb = nc.dram_tensor("lnb", [128, D], F32, kind="ExternalInput").ap()
    out = nc.dram_tensor("xo", [TOK, D], F32, kind="ExternalOutput").ap()
    with ExitStack() as es:
        A, PS, S = _setup(nc, es)
        phase_outproj_ln(S, A, PS, oT, w, xres, lng, lnb, out, TOK, DF, D, "c")
        S.emit()
    return nc


LAM_INIT1 = float(0.8 - 0.6 * np.exp(-0.3 * 1))
TVL = 1152


def host_bias_onehot():
    oh = np.zeros((33, TVL), np.float32)
    for m in range(TVL):
        d = m - 512
        if d < 0:
            oh[32, m] = 1.0
            continue
        if d < 16:
            b = d
        else:
            nf = np.float32(max(d, 1))
            v = np.log(nf / np.float32(16)) / np.float32(np.log(128 / 16)) * np.float32(16)
            b = min(16 + int(np.float32(v)), 31)
        oh[b, m] = 1.0
    return oh


def phase_attn(S, A, PS, C, hT, rbrep_d, oh_d, ch_d, lamv_d, subw_d, fscr, out_d, T, NHL, tag):
    ident, ones, onesm = C["ident"], C["ones"], C["onesm"]
    NQG = (T + 511) // 512
    QW = min(512, T)
    NB = T // 128
    scale = float(128 ** -0.5)
    qk = [A.alloc([4, T], BF16) for _ in range(2)]
    Vt = A.alloc([NB, 256], BF16)
    vst = [A.alloc([2, QW]) for _ in range(2)]
    zt = [A.alloc([2, QW]) for _ in range(2)]
    BT = A.alloc([5, QW])
    PT = [A.alloc([QW], BF16) for _ in range(3)]
    tmp = [A.alloc([QW]) for _ in range(2)]
    on0 = A.alloc([2, QW]); dd = A.alloc([2, QW]); sq = A.alloc([2, QW])
    rz = A.alloc([QW]); t1 = A.alloc([QW]); rstd = A.alloc([QW])
    ob = [A.alloc([2, QW], BF16) for _ in range(2)]
    ones_bf = A.alloc([128], BF16)
    onesm2 = A.alloc([128])
    rbrep = A.alloc([NHL * 128]); oh = A.alloc([TVL]); tvb = A.alloc([TVL])
    ch = A.alloc([NHL]); lamv = A.alloc([4]); subw = A.alloc([2]); lam = A.alloc([4])
    CP(S, "dve", ones_bf, ones, ["ones"], ["ones_bf"])
    TS(S, "dve", onesm2, onesm, 0.5, ALU.mult, ["onesm"], ["onesm2"])
    S.dma("sp", rbrep[0:33, :], rbrep_d[:, :], writes=["rbrep"])
    S.dma("sp", oh[0:33, :], oh_d[:, :], writes=["oh"])
    S.dma("sp", ch, ch_d[:, :], writes=["ch"])
    S.dma("sp", lamv, lamv_d[:, :], writes=["lamv"])
    S.dma("sp", subw, subw_d[:, :], writes=["subw"])
    TT(S, "dve", lam[:, 0:1], lamv[:, 0:1], lamv[:, 1:2], ALU.mult, ["lamv"], ["lam"])
    TT(S, "dve", lam[:, 1:2], lamv[:, 2:3], lamv[:, 3:4], ALU.mult, ["lamv", "lam"], ["lam"])
    MM(S, PS[7][:, 0:2], ones, lam[:, 0:2], ["ones", "lam"], ["ps7"])
    ACT(S, lam[:, 2:4], PS[7][:, 0:2], AF.Exp, ["ps7"], ["lam2"])
    TT(S, "dve", lam[:, 0:1], lam[:, 3:4], lam[:, 2:3], ALU.subtract, ["lam2", "lam"], ["lam"])
    TS(S, "dve", lam[:, 0:1], lam[:, 0:1], -LAM_INIT1, ALU.add, ["lam"], ["neglam"])
    TS(S, "dve", subw, subw, 1.0 - LAM_INIT1, ALU.mult, ["subw"], ["subw"])
    neglam = lam[:, 0:1]

    def load_qk(h):
        b = qk[h % 2]
        S.dma("pool", b, hT[h * 1024:h * 1024 + 512, :].rearrange("(j p) t -> p j t", p=128),
              reads=[(tag, "hT", h * 8 + j, c) for j in range(4) for c in range(NB)], writes=[("qk", h % 2)])

    load_qk(0)
    n_st = 0
    n_pt = 0
    n_tmp = 0
    n_ob = 0
    for h in range(NHL):
        if h + 1 < NHL:
            load_qk(h + 1)
        qkb = qk[h % 2]
        qkk = ("qk", h % 2)
        for i, (c0, n) in enumerate([(0, 512), (512, 512), (1024, TVL - 1024)]):
            MM(S, PS[7][:, 0:n], rbrep[0:33, h * 128:(h + 1) * 128], oh[0:33, c0:c0 + n], ["rbrep", "oh"], ["ps7"])
            CP(S, "dve", tvb[:, c0:c0 + n], PS[7][:, 0:n], ["ps7"], ["tvb"])
        S.dma("sp", fscr[h], tvb, reads=["tvb"], writes=[("fscr", h)])
        fl = fscr[h].rearrange("p l -> (p l)")
        for r in range(-1, 4):
            off = 512 - 128 * r
            src = bass.AP(fl.tensor, fl.offset + off, [[TVL - 1, 128], [1, QW]])
            S.dma("sp", BT[:, r + 1, :], src, reads=[("fscr", h)], writes=[("BT", r + 1)])
        for piece in range(NQG):
            vi = piece % 2
            S.dma("sp", vst[vi], hT[h * 1024 + 512:h * 1024 + 768, piece * QW:(piece + 1) * QW].rearrange("(j p) t -> p j t", p=128),
                  reads=[(tag, "hT", h * 8 + 4 + j, piece * (QW // 128) + c) for j in range(2) for c in range(QW // 128)], writes=[("vst", vi)])
            for tb in range(QW // 128):
                gb = piece * (QW // 128) + tb
                pv = PS[5].rearrange("p (a b) -> p a b", a=4)
                for eb in range(2):
                    TR(S, pv[:, eb, :], vst[vi][:, eb, tb * 128:(tb + 1) * 128], ident, [("vst", vi)], ["ps5"])
                CP(S, "dve" if gb % 2 else "act", Vt[:, gb, :].rearrange("p (a b) -> p a b", a=2), pv[:, 0:2, :], ["ps5"], [("Vt", gb)])
        for qg in range(NQG):
            qs = slice(qg * QW, (qg + 1) * QW)
            zi = qg % 2
            S.dma("sp", zt[zi], hT[h * 1024 + 768:h * 1024 + 1024, qs].rearrange("(j p) t -> p j t", p=128),
                  reads=[(tag, "hT", h * 8 + 6 + j, qg * (QW // 128) + c) for j in range(2) for c in range(QW // 128)], writes=[("zt", zi)])
            ACT(S, zt[zi], zt[zi], AF.Silu, [("zt", zi)], [("zt", zi)])
            nkb = (qg + 1) * (QW // 128)
            for p in range(2):
                for kb in range(nkb):
                    si = n_st % 2
                    n_st += 1
                    st = PS[si][:, 0:QW]
                    MM(S, st, qkb[:, 2 + p, kb * 128:(kb + 1) * 128], qkb[:, p, qs], [qkk], [("st", si)])
                    pi = n_pt % 3
                    n_pt += 1
                    r = kb - qg * (QW // 128)
                    if r <= -2:
                        ACT(S, PT[pi], st, AF.Exp, [("st", si), "ch"], [("PT", pi)], scale=scale, bias=ch[:, h:h + 1])
                    else:
                        ti = n_tmp % 2
                        n_tmp += 1
                        STT(S, tmp[ti], st, scale, BT[:, r + 1, :], ALU.mult, ALU.add, [("st", si), ("BT", r + 1)], [("tmp", ti)])
                        ACT(S, PT[pi], tmp[ti], AF.Exp, [("tmp", ti)], [("PT", pi)])
                    first, last = (kb == 0), (kb == nkb - 1)
                    MM(S, PS[2][:, 0:QW], Vt[:, kb, 0:128], PT[pi], [("Vt", kb), ("PT", pi)], ["ps2"], start=first, stop=last)
                    MM(S, PS[3][:, 0:QW], Vt[:, kb, 128:256], PT[pi], [("Vt", kb), ("PT", pi)], ["ps3"], start=first, stop=last)
                    MM(S, PS[4][:, 0:QW], ones_bf, PT[pi], ["ones_bf", ("PT", pi)], ["ps4"], start=first, stop=last)
                S.op("dve", lambda e: e.reciprocal(rz, PS[4][:, 0:QW]), reads=["ps4"], writes=["rz"])
                for eb in range(2):
                    src = PS[2 + eb][:, 0:QW]
                    if p == 0:
                        TT(S, "dve", on0[:, eb, :], src, rz, ALU.mult, ["ps%d" % (2 + eb), "rz"], [("on0", eb)])
                    else:
                        TT(S, "dve", t1, src, rz, ALU.mult, ["ps%d" % (2 + eb), "rz"], ["t1"])
                        STT(S, dd[:, eb, :], t1, neglam, on0[:, eb, :], ALU.mult, ALU.add, ["t1", "neglam", ("on0", eb)], [("dd", eb)])
            ACT(S, sq, dd, AF.Square, [("dd", 0), ("dd", 1)], ["sq"])
            MM(S, PS[6][:, 0:QW], onesm2, sq[:, 0, :], ["onesm2", "sq"], ["ps6"], start=True, stop=False)
            MM(S, PS[6][:, 0:QW], onesm2, sq[:, 1, :], ["onesm2", "sq"], ["ps6"], start=False, stop=True)
            ACT(S, rstd, PS[6][:, 0:QW], AF.Ln, ["ps6"], ["rstd"], bias=1e-5)
            ACT(S, rstd, rstd, AF.Exp, ["rstd"], ["rstd"], scale=-0.5)
            oi = n_ob % 2
            n_ob += 1
            for eb in range(2):
                TT(S, "pool", dd[:, eb, :], dd[:, eb, :], rstd, ALU.mult, [("dd", eb), "rstd"], [("dd", eb)])
                STT(S, ob[oi][:, eb, :], dd[:, eb, :], subw[:, eb:eb + 1], zt[zi][:, eb, :], ALU.mult, ALU.mult,
                    [("dd", eb), "subw", ("zt", zi)], [("ob", oi)])
            S.dma("sp", out_d[h * 256:(h + 1) * 256, qs].rearrange("(j p) t -> p j t", p=128), ob[oi], reads=[("ob", oi)],
                  writes=[(tag, "oT", h, qg)], final=True)


def build_l1ab(T, D, NHL, TT=1024):
    from contextlib import ExitStack
    nc = bass.Bass("TRN2", target_bir_lowering=False)
    NCOL = NHL * 1024
    x = nc.dram_tensor("x", [T, D], F32, kind="ExternalInput").ap()
    w = nc.dram_tensor("w", [D, NCOL], F32, kind="ExternalInput").ap()
    cst = nc.dram_tensor("cst", [128, 7 * 128], F32, kind="ExternalInput").ap()
    rbrep = nc.dram_tensor("rbrep", [33, NHL * 128], F32, kind="ExternalInput").ap()
    oh = nc.dram_tensor("oh", [33, TVL], F32, kind="ExternalInput").ap()
    ch = nc.dram_tensor("ch", [128, NHL], F32, kind="ExternalInput").ap()
    lamv = nc.dram_tensor("lamv", [128, 4], F32, kind="ExternalInput").ap()
    subw = nc.dram_tensor("subw", [128, 2], F32, kind="ExternalInput").ap()
    out = nc.dram_tensor("oT", [NHL * 256, T], BF16, kind="ExternalOutput").ap()
    hT = nc.dram_tensor("hT", [NCOL, T], F32).ap()
    fscr = nc.dram_tensor("fscr", [NHL, 128, TVL], F32).ap()
    with ExitStack() as es:
        A, PS, S = _setup(nc, es)
        C = load_consts(S, A, cst)
        base = A.off
        phase_gemm_fm(S, A, PS, C["ident"], x, w, hT, T, D, NCOL, TT, "l1")
        S.barrier()
        A.reset(base)
        phase_attn(S, A, PS, C, hT, rbrep, oh, ch, lamv, subw, fscr, out, T, NHL, "l1")
        S.emit()
    return nc


def prep_l1(inp, xb, heads, NH_TOT=16):
    QW_ = 2 * NH_TOT * 128
    w_in = inp["b_w_in"][0]
    cols = []
    for h in heads:
        cols.append(np.arange(h * 256, h * 256 + 256))
        cols.append(np.arange(QW_ + h * 256, QW_ + h * 256 + 256))
        cols.append(np.arange(2 * QW_ + h * 256, 2 * QW_ + h * 256 + 256))
        cols.append(np.arange(3 * QW_ + h * 256, 3 * QW_ + h * 256 + 256))
    cols = np.concatenate(cols)
    rb = inp["rel_bias"]
    rbrep = np.zeros((33, len(heads) * 128), np.float32)
    for i, h in enumerate(heads):
        rbrep[0:32, i * 128:(i + 1) * 128] = rb[:, h][:, None]
        rbrep[32, i * 128:(i + 1) * 128] = NEGBIG
    ch = np.ascontiguousarray(np.broadcast_to(rb[31, heads][None, :], (128, len(heads)))).astype(np.float32)
    lamv = np.stack([inp["b_lam_q1"][0], inp["b_lam_k1"][0], inp["b_lam_q2"][0], inp["b_lam_k2"][0]], axis=1).astype(np.float32)
    subw = np.ascontiguousarray(inp["b_subln_w"][0].reshape(2, 128).T).astype(np.float32)
    return {"x": np.ascontiguousarray(xb), "w": np.ascontiguousarray(w_in[:, cols]), "cst": host_consts(), "rbrep": rbrep,
            "oh": host_bias_onehot(), "ch": ch, "lamv": np.ascontiguousarray(lamv), "subw": subw}


N_CORES = 8
_T, _D = 4096, 4096


def _run(nc, in_maps):
    res = run_bass_kernel_spmd(nc, in_maps, core_ids=list(range(N_CORES)))
    return res.results


def _outproj_launch(nc_c, oT_cores, xin, w_out, g, b):
    B = xin.shape[0]
    TOK = _T // 2
    lng = np.ascontiguousarray(np.broadcast_to(g[None, :], (128, _D))).astype(np.float32)
    lnb = np.ascontiguousarray(np.broadcast_to(b[None, :], (128, _D))).astype(np.float32)
    maps = []
    for c in range(N_CORES):
        bb, th = c // 2, c % 2
        full = np.concatenate([oT_cores[2 * bb], oT_cores[2 * bb + 1]], axis=0)
        maps.append({"oT": np.ascontiguousarray(full[:, th * TOK:(th + 1) * TOK]), "w": w_out,
                     "xres": np.ascontiguousarray(xin[bb, th * TOK:(th + 1) * TOK, :]), "lng": lng, "lnb": lnb})
    res = _run(nc_c, maps)
    out = np.empty((B, _T, _D), np.float32)
    for c in range(N_CORES):
        bb, th = c // 2, c % 2
        out[bb, th * TOK:(th + 1) * TOK, :] = res[c]["xo"]
    return out


def kernel(**inputs):
    inp = {k: np.asarray(v) for k, v in inputs.items()}
    x = inp["x"]
    nc0 = build_l0ab(_T, _D, 16)
    maps = [prep_l0(inp, c // 2, list(range((c % 2) * 16, (c % 2) * 16 + 16)), _T) for c in range(N_CORES)]
    r0 = _run(nc0, maps)
    del maps
    nc_c = build_c(_T // 2, _D, _D)
    x1 = _outproj_launch(nc_c, [np.asarray(r["oT"]) for r in r0], x, np.ascontiguousarray(inp["a_w_out"][0]), inp["ln_g"][0], inp["ln_b"][0])
    nc1 = build_l1ab(_T, _D, 8)
    maps = [prep_l1(inp, x1[c // 2], list(range((c % 2) * 8, (c % 2) * 8 + 8))) for c in range(N_CORES)]
    r1 = _run(nc1, maps)
    del maps
    nc_c2 = build_c(_T // 2, _D, _D)
    x2 = _outproj_launch(nc_c2, [np.asarray(r["oT"]) for r in r1], x1, np.ascontiguousarray(inp["b_w_out"][0]), inp["ln_g"][1], inp["ln_b"][1])
    return x2
```
